# Optimizing a Trainium2 kernel written in Bass

```python
import jax, jax.numpy as jnp
from jax import lax
import numpy as np

D_MODEL = 2048
BATCH = 1
SEQ = 8192
DEPTH = 2

ATT_HEAD_DIM = 64
ATT_Q_HEADS = 16
ATT_KV_HEADS = 4
ATT_GROUP = ATT_Q_HEADS // ATT_KV_HEADS
WINDOW = 128
ATT_BLOCK = WINDOW
ATT_WIDTH = ATT_Q_HEADS * ATT_HEAD_DIM
KV_WIDTH = ATT_KV_HEADS * ATT_HEAD_DIM

POOL_WINDOWS = (2, 4, 8, 16)
POOL_GROUPS = len(POOL_WINDOWS)
POOL_WIDTH = D_MODEL // 2
POOL_GROUP_DIM = POOL_WIDTH // POOL_GROUPS

D_INNER = D_MODEL
SSM_HEAD_DIM = 64
SSM_HEADS = D_INNER // SSM_HEAD_DIM
SSM_GROUPS = 4
HEADS_PER_GROUP = SSM_HEADS // SSM_GROUPS
D_STATE = 128
CONV_K = 4
CHUNK = 128
CONV_CH = D_INNER + 2 * SSM_GROUPS * D_STATE

N_BRANCH = 3
D_FF = -(-8 * D_MODEL // (3 * 256)) * 256
EPS = 1e-6

IN_WIDTHS = (ATT_WIDTH, KV_WIDTH, KV_WIDTH, POOL_WIDTH, D_INNER, CONV_CH, SSM_HEADS, N_BRANCH * D_MODEL)
IN_COLS = sum(IN_WIDTHS)

kernel_name = 'hybrid_gated_swa_pool_ssd_block'


def split_points(widths):
    pts, acc = [], 0
    for w in widths[:-1]:
        acc += w
        pts.append(acc)
    return pts


def rmsnorm(x, w):
    xf = x.astype(jnp.float32)
    y = xf * lax.rsqrt(jnp.mean(xf * xf, axis=-1, keepdims=True) + EPS)
    return (y * w.astype(jnp.float32)).astype(x.dtype)


def sink_window_attention(q, k, v, sink):
    b, l = q.shape[0], q.shape[1]
    nb = l // ATT_BLOCK
    qb = q.reshape(b, nb, ATT_BLOCK, ATT_KV_HEADS, ATT_GROUP, ATT_HEAD_DIM)
    kb = k.reshape(b, nb, ATT_BLOCK, ATT_KV_HEADS, ATT_HEAD_DIM)
    vb = v.reshape(b, nb, ATT_BLOCK, ATT_KV_HEADS, ATT_HEAD_DIM)

    def with_prev(t):
        prev = jnp.pad(t[:, :-1], ((0, 0), (1, 0), (0, 0), (0, 0), (0, 0)))
        return jnp.concatenate([prev, t], axis=2)

    kk, vv = with_prev(kb), with_prev(vb)
    scores = jnp.einsum('bnqkgd,bnskd->bnkgqs', qb, kk).astype(jnp.float32) * (ATT_HEAD_DIM ** -0.5)
    blk = jnp.arange(nb)[:, None, None]
    qpos = blk * ATT_BLOCK + jnp.arange(ATT_BLOCK)[None, :, None]
    kpos = (blk - 1) * ATT_BLOCK + jnp.arange(2 * ATT_BLOCK)[None, None, :]
    diff = qpos - kpos
    mask = (diff >= 0) & (diff < WINDOW) & (kpos >= 0)
    scores = jnp.where(mask[None, :, None, None], scores, -jnp.inf)
    sink_l = sink.astype(jnp.float32).reshape(ATT_KV_HEADS, ATT_GROUP)[None, None, :, :, None, None]
    sink_l = jnp.broadcast_to(sink_l, scores.shape[:-1] + (1,))
    probs = jax.nn.softmax(jnp.concatenate([scores, sink_l], axis=-1), axis=-1)[..., :-1]
    out = jnp.einsum('bnkgqs,bnskd->bnqkgd', probs.astype(v.dtype), vv)
    return out.reshape(b, l, ATT_WIDTH)


def multiscale_pool(u, pool_w, pool_scale):
    b, l, _ = u.shape
    ug = u.reshape(b, l, POOL_GROUPS, POOL_GROUP_DIM).astype(jnp.float32)
    cs = jnp.cumsum(ug, axis=1)
    t = jnp.arange(l)
    means = []
    for gi, w in enumerate(POOL_WINDOWS):
        c = cs[:, :, gi]
        shifted = jnp.pad(c, ((0, 0), (w, 0), (0, 0)))[:, :l]
        cnt = jnp.minimum(t + 1, w).astype(jnp.float32)[None, :, None]
        means.append((c - shifted) / cnt)
    mixed = (jnp.stack(means, axis=2) - ug).astype(u.dtype)
    y = jnp.einsum('blgc,gcd->blgd', mixed, pool_w).reshape(b, l, POOL_WIDTH)
    return y * pool_scale


def ssd_mixer(z, xbc, dt_raw, conv_w, conv_b, dt_bias, a_log, d_skip, norm_w):
    b, l, _ = xbc.shape
    xbc = lax.conv_general_dilated(xbc, conv_w[:, None, :].astype(xbc.dtype), window_strides=(1,),
                                   padding=[(CONV_K - 1, 0)], dimension_numbers=('NWC', 'WIO', 'NWC'),
                                   feature_group_count=CONV_CH) + conv_b
    xbc = jax.nn.silu(xbc)
    nc = l // CHUNK
    xs = xbc[..., :D_INNER].astype(jnp.float32).reshape(b, nc, CHUNK, SSM_GROUPS, HEADS_PER_GROUP, SSM_HEAD_DIM)
    bm = xbc[..., D_INNER:D_INNER + SSM_GROUPS * D_STATE].astype(jnp.float32).reshape(b, nc, CHUNK, SSM_GROUPS, D_STATE)
    cm = xbc[..., D_INNER + SSM_GROUPS * D_STATE:].astype(jnp.float32).reshape(b, nc, CHUNK, SSM_GROUPS, D_STATE)
    dt = jax.nn.softplus(dt_raw.astype(jnp.float32) + dt_bias.astype(jnp.float32))
    dt = dt.reshape(b, nc, CHUNK, SSM_GROUPS, HEADS_PER_GROUP)
    a = -jnp.exp(a_log.astype(jnp.float32)).reshape(SSM_GROUPS, HEADS_PER_GROUP)
    a_cs = jnp.cumsum(dt * a, axis=2)
    xdt = xs * dt[..., None]
    seg = a_cs[:, :, :, None] - a_cs[:, :, None, :]
    causal = jnp.tril(jnp.ones((CHUNK, CHUNK), dtype=bool))
    decay = jnp.exp(jnp.where(causal[:, :, None, None], seg, -jnp.inf))
    cb = jnp.einsum('bclgn,bcsgn->bclsg', cm, bm)
    y_diag = jnp.einsum('bclsg,bclsge,bcsgep->bclgep', cb, decay, xdt)
    decay_to_end = jnp.exp(a_cs[:, :, -1:] - a_cs)
    states = jnp.einsum('bcsgn,bcsge,bcsgep->bcgepn', bm, decay_to_end, xdt)
    chunk_decay = jnp.exp(a_cs[:, :, -1])

    def step(h, inp):
        st, dec = inp
        return h * dec[..., None, None] + st, h

    h0 = jnp.zeros((b, SSM_GROUPS, HEADS_PER_GROUP, SSM_HEAD_DIM, D_STATE), jnp.float32)
    _, prev = lax.scan(step, h0, (jnp.moveaxis(states, 1, 0), jnp.moveaxis(chunk_decay, 1, 0)))
    prev = jnp.moveaxis(prev, 0, 1)
    y_off = jnp.einsum('bclgn,bcgepn,bclge->bclgep', cm, prev, jnp.exp(a_cs))
    y = y_diag + y_off + d_skip.astype(jnp.float32).reshape(SSM_GROUPS, HEADS_PER_GROUP)[:, :, None] * xs
    y = y.reshape(b, l, D_INNER)
    g = (y * jax.nn.silu(z.astype(jnp.float32))).reshape(b, l, SSM_GROUPS, D_INNER // SSM_GROUPS)
    g = g * lax.rsqrt(jnp.mean(g * g, axis=-1, keepdims=True) + EPS)
    return (g.reshape(b, l, D_INNER) * norm_w.astype(jnp.float32)).astype(z.dtype)


def setup_inputs(seed: int = 0) -> dict:
    key = jax.random.key(seed)
    ks = jax.random.split(key, 24)
    f32 = jnp.float32

    def nrm(k, shape, scale):
        return jax.random.normal(k, shape, f32) * scale

    dt0 = jnp.exp(jax.random.uniform(ks[6], (DEPTH, SSM_HEADS), f32, np.log(1e-3), np.log(1e-1)))
    dt_bias = dt0 + jnp.log(-jnp.expm1(-dt0))
    a_log = jnp.log(jax.random.uniform(ks[7], (DEPTH, SSM_HEADS), f32, 1.0, 16.0))
    return {
        'x': nrm(ks[0], (BATCH, SEQ, D_MODEL), 1.0),
        'ln1_w': 1.0 + nrm(ks[1], (DEPTH, D_MODEL), 0.05),
        'w_in': nrm(ks[2], (DEPTH, D_MODEL, IN_COLS), D_MODEL ** -0.5),
        'attn_sink': nrm(ks[3], (DEPTH, ATT_Q_HEADS), 0.5),
        'conv_w': nrm(ks[4], (DEPTH, CONV_K, CONV_CH), CONV_K ** -0.5),
        'conv_b': nrm(ks[5], (DEPTH, CONV_CH), 0.02),
        'dt_bias': dt_bias,
        'a_log': a_log,
        'd_skip': 1.0 + nrm(ks[8], (DEPTH, SSM_HEADS), 0.1),
        'ssm_norm_w': 1.0 + nrm(ks[9], (DEPTH, D_INNER), 0.05),
        'pool_w': nrm(ks[10], (DEPTH, POOL_GROUPS, POOL_GROUP_DIM, POOL_GROUP_DIM), POOL_GROUP_DIM ** -0.5),
        'pool_scale': 1.0 + nrm(ks[11], (DEPTH, POOL_WIDTH), 0.1),
        'w_attn_br': nrm(ks[12], (DEPTH, ATT_WIDTH, D_MODEL), ATT_WIDTH ** -0.5),
        'w_pool_br': nrm(ks[13], (DEPTH, POOL_WIDTH, D_MODEL), POOL_WIDTH ** -0.5),
        'w_ssm_br': nrm(ks[14], (DEPTH, D_INNER, D_MODEL), D_INNER ** -0.5),
        'w_out': nrm(ks[15], (DEPTH, D_MODEL, D_MODEL), D_MODEL ** -0.5),
        'ln2_w': 1.0 + nrm(ks[16], (DEPTH, D_MODEL), 0.05),
        'w_gate_up': nrm(ks[17], (DEPTH, D_MODEL, 2 * D_FF), D_MODEL ** -0.5),
        'w_down': nrm(ks[18], (DEPTH, D_FF, D_MODEL), D_FF ** -0.5),
        'final_w': 1.0 + nrm(ks[19], (D_MODEL,), 0.05),
    }


def reference(x, ln1_w, w_in, attn_sink, conv_w, conv_b, dt_bias, a_log, d_skip, ssm_norm_w,
              pool_w, pool_scale, w_attn_br, w_pool_br, w_ssm_br, w_out, ln2_w, w_gate_up,
              w_down, final_w):
    b, l, _ = x.shape
    pts = split_points(IN_WIDTHS)
    for i in range(DEPTH):
        h = rmsnorm(x, ln1_w[i])
        proj = h @ w_in[i]
        q, k, v, u, z, xbc, dt_raw, gate_logits = jnp.split(proj, pts, axis=-1)
        att = sink_window_attention(q.reshape(b, l, ATT_Q_HEADS, ATT_HEAD_DIM),
                                    k.reshape(b, l, ATT_KV_HEADS, ATT_HEAD_DIM),
                                    v.reshape(b, l, ATT_KV_HEADS, ATT_HEAD_DIM), attn_sink[i])
        pool = multiscale_pool(u, pool_w[i], pool_scale[i])
        ssm = ssd_mixer(z, xbc, dt_raw, conv_w[i], conv_b[i], dt_bias[i], a_log[i], d_skip[i], ssm_norm_w[i])
        gates = jax.nn.sigmoid(gate_logits.astype(jnp.float32)).astype(x.dtype).reshape(b, l, N_BRANCH, D_MODEL)
        merged = (gates[:, :, 0] * (att @ w_attn_br[i])
                  + gates[:, :, 1] * (pool @ w_pool_br[i])
                  + gates[:, :, 2] * (ssm @ w_ssm_br[i]))
        x = x + merged @ w_out[i]
        h2 = rmsnorm(x, ln2_w[i])
        gu = h2 @ w_gate_up[i]
        x = x + (jax.nn.silu(gu[..., :D_FF]) * gu[..., D_FF:]) @ w_down[i]
    return rmsnorm(x, final_w)
```

```python
import sys
import numpy as np
import concourse.bass as bass
import concourse.mybir as mybir
from concourse.bass_utils import run_bass_kernel_spmd

F32 = mybir.dt.float32
BF16 = mybir.dt.bfloat16
AF = mybir.ActivationFunctionType
ALU = mybir.AluOpType

NCORE = 8
NSEG = 2
T = 512
HT = 128
TT = 640
D = 2048
EPS = 1e-6
LN1, LN2, FIN, CW, CB, DSK, NRM, PSC, DTB, ALOG, SINK, NV = 0, 16, 32, 48, 144, 168, 184, 200, 208, 240, 272, 288
SB_BASE = 16512
SB_END = 229376


class Buf:
    def __init__(self, t, name="", psum=False):
        self.t = t
        self.name = name
        self.psum = psum

    def __getitem__(self, k):
        return self.t[k]


class Prog:
    ENG = ["pe", "act", "dve", "pool", "sp"]

    def __init__(self, nc):
        self.nc = nc
        self.ops = []

    def op(self, eng, fn, reads=(), writes=(), dma=False):
        rd = [self._k(x) for x in reads]
        wr = [self._k(x) for x in writes]
        wr = wr + [(b, None) for (b, k) in rd if getattr(b, "psum", False)]
        rd = [(b, k) for (b, k) in rd if not getattr(b, "psum", False)]
        self.ops.append(dict(eng=eng, fn=fn, reads=rd, writes=wr, dma=dma, barrier=False))

    def barrier(self):
        self.ops.append(dict(barrier=True))

    @staticmethod
    def _k(x):
        if isinstance(x, tuple):
            return x
        return (x, None)

    def emit(self):
        nc = self.nc
        ops = self.ops
        eng_cnt = {e: 0 for e in self.ENG}
        dma_sem_of = {}
        dma_cnt = {}
        st = {}
        n_dma_sems = 0
        for i, o in enumerate(ops):
            if o["barrier"]:
                o["snap"] = (dict(eng_cnt), dict(dma_cnt))
                continue
            deps = set()
            for (b, k) in o["reads"]:
                s = st.setdefault(b, dict(ww=None, wr=[], kw={}, kr={}))
                if s["ww"] is not None:
                    deps.add(s["ww"])
                if k is None:
                    deps.update(s["kw"].values())
                elif k in s["kw"]:
                    deps.add(s["kw"][k])
            for (b, k) in o["writes"]:
                s = st.setdefault(b, dict(ww=None, wr=[], kw={}, kr={}))
                if s["ww"] is not None:
                    deps.add(s["ww"])
                deps.update(s["wr"])
                if k is None:
                    deps.update(s["kw"].values())
                    for l in s["kr"].values():
                        deps.update(l)
                else:
                    if k in s["kw"]:
                        deps.add(s["kw"][k])
                    deps.update(s["kr"].get(k, []))
            deps.discard(i)
            o["deps"] = deps
            for (b, k) in o["writes"]:
                s = st[b]
                if k is None:
                    s["ww"] = i
                    s["wr"] = []
                    s["kw"] = {}
                    s["kr"] = {}
                else:
                    s["kw"][k] = i
                    s["kr"][k] = []
            for (b, k) in o["reads"]:
                s = st[b]
                if k is None:
                    s["wr"].append(i)
                else:
                    s["kr"].setdefault(k, []).append(i)
            if o["dma"]:
                key = o["writes"][0][0]
                if key not in dma_sem_of:
                    dma_sem_of[key] = n_dma_sems
                    n_dma_sems += 1
                kk = dma_sem_of[key]
                dma_cnt[kk] = dma_cnt.get(kk, 0) + 16
                o["tok"] = ("dma", kk, dma_cnt[kk])
            else:
                eng_cnt[o["eng"]] += 1
                o["tok"] = ("eng", o["eng"], eng_cnt[o["eng"]])
        print("ops", len(ops), "eng_cnt", eng_cnt, "dma sems", n_dma_sems, file=sys.stderr)

        from contextlib import ExitStack
        with ExitStack() as es:
            esem = {e: es.enter_context(nc.semaphore("s_" + e)) for e in self.ENG}
            dsem = [es.enter_context(nc.semaphore("d_%d" % k)) for k in range(n_dma_sems)]
            block = es.enter_context(nc.Block())

            def run_engine(ename, eng):
                waited = {}

                def wait(key, sem, val):
                    if val > 0 and waited.get(key, 0) < val:
                        eng.wait_ge(sem, val)
                        waited[key] = val

                for o in ops:
                    if o["barrier"]:
                        ec, dc = o["snap"]
                        for e2, v in ec.items():
                            if e2 != ename or ename != "pe":
                                wait(("eng", e2), esem[e2], v)
                        for k2, v in dc.items():
                            wait(("dma", k2), dsem[k2], v)
                        continue
                    if o["eng"] != ename:
                        continue
                    for d in sorted(o["deps"]):
                        tok = ops[d]["tok"]
                        if tok[0] == "eng":
                            if tok[1] == ename and ename == "pe" and not o["dma"]:
                                continue
                            wait(("eng", tok[1]), esem[tok[1]], tok[2])
                        else:
                            wait(("dma", tok[1]), dsem[tok[1]], tok[2])
                    ins = o["fn"](eng)
                    tok = o["tok"]
                    if tok[0] == "eng":
                        ins.then_inc(esem[ename], 1)
                    else:
                        ins.then_inc(dsem[tok[1]], 16)
                for o in ops:
                    if (not o["barrier"]) and o["eng"] == ename and o["dma"]:
                        tok = o["tok"]
                        wait(("dma", tok[1]), dsem[tok[1]], tok[2])

            @block.tensor
            def _(e):
                run_engine("pe", e)

            @block.scalar
            def _(e):
                run_engine("act", e)

            @block.vector
            def _(e):
                run_engine("dve", e)

            @block.gpsimd
            def _(e):
                run_engine("pool", e)

            @block.sync
            def _(e):
                run_engine("sp", e)


class Rot:
    def __init__(self, bufs):
        self.bufs = bufs
        self.i = 0

    def next(self):
        b = self.bufs[self.i % len(self.bufs)]
        self.i += 1
        return b


def bc(ap, shape, axis):
    return ap.unsqueeze(axis).to_broadcast(shape)


class StopBuild(Exception):
    pass


STOP = [None]


_chk_cnt = {}


def chk(name):
    if STOP[0] is None:
        return
    want = STOP[0].split("@")
    k = int(want[1]) if len(want) > 1 else 0
    if want[0] == name:
        c = _chk_cnt.get(name, 0)
        _chk_cnt[name] = c + 1
        if c == k:
            raise StopBuild()


def build_program(mode, last):
    p2 = mode == "p2"
    nc = bass.Bass("TRN2", target_bir_lowering=False)
    P = Prog(nc)

    def din(name, shape):
        return nc.dram_tensor(name, list(shape), F32, kind="ExternalInput").ap()

    def dout(name, shape):
        return nc.dram_tensor(name, list(shape), F32, kind="ExternalOutput").ap()

    xT_d = din("xT", [NSEG, D, TT])
    vec_d = din("vec", [128, NV])
    w_ssd_d = din("w_ssd", [12, 128, 4096])
    w_dt_d = din("w_dt", [128, 512])
    if p2:
        w_z_d = din("w_z", [8, 128, 4096])
        w_q_d = din("w_q", [4, 128, 4096])
        w_k_d = din("w_k", [2, 128, 4096])
        w_v_d = din("w_v", [2, 128, 4096])
        w_u_d = din("w_u", [4, 128, 4096])
        w_gate_d = din("w_gate", [16, 128, 6144])
        w_br_d = din("w_br", [16, 128, 4096])
        w_o_d = din("w_o", [8, 128, 4096])
        w_gu_d = din("w_gu", [44, 128, 4096])
        w_dn_d = din("w_dn", [16, 128, 5632])
        w_pool_d = din("w_pool", [128, 2048])
        SS_d = din("SS", [NSEG, 15, 128, 2048])
        DD_d = din("DD", [NSEG, 15, 128, 32])
        segv_d = din("segv", [NSEG, 128, 65])
        y_d = dout("yT", [NSEG, D, T])
    else:
        S_out_d = dout("Sout", [NSEG, 128, 2048])
        D_out_d = dout("Dout", [NSEG, 128, 32])

    cur = [SB_BASE]
    nid = [0]

    def sb(shape, dt, name=None):
        nbytes = int(np.prod(shape[1:])) * (4 if dt == F32 else 2)
        nbytes = (nbytes + 31) // 32 * 32
        off = cur[0]
        cur[0] += nbytes
        assert cur[0] <= SB_END, ("SBUF overflow", name, cur[0])
        nid[0] += 1
        nm = "%s_%d" % (name or "t", nid[0])
        return Buf(nc.alloc_sbuf_tensor_at(nm, list(shape), dt, offset=off), nm)

    xT = sb([128, 16, TT], F32, "xT")
    hT = sb([128, 16, TT], BF16, "hT")
    wbufs = Rot([sb([128, 6144], BF16, "wbuf") for _ in range(3)])
    ones_bf = sb([128, 128], BF16, "ones")
    ident_bf = sb([128, 128], BF16, "ident")
    mask2 = sb([128, 2, 4, 128], BF16, "mask2")
    neg4 = sb([128, 4, 128], BF16, "neg4")
    vec = sb([128, NV], F32, "vec")
    epsc = sb([128, 2], F32, "epsc")
    ssmT = sb([128, 16, T], BF16, "ssmT")
    arena0 = cur[0]
    U_bf = mask2

    from contextlib import ExitStack
    es = ExitStack()
    with es:
        psA = Rot([Buf(es.enter_context(nc.psum_tensor("psA%d" % i, [128, 512], F32)), "psA%d" % i, psum=True) for i in range(3)])
        psB = Rot([Buf(es.enter_context(nc.psum_tensor("psB%d" % i, [128, 1024], F32)), "psB%d" % i, psum=True) for i in range(2)])
        psT = Rot([Buf(es.enter_context(nc.psum_tensor("psT", [128, 512], BF16)), "psT", psum=True)])

        cur[0] = arena0
        mtmp = sb([128, 4, 128], F32, "mtmp")
        P.op("sp", lambda e: e.dma_start(out=vec[:], in_=vec_d), writes=[vec], dma=True)
        P.op("dve", lambda e: e.memset(ones_bf[:], 1.0), writes=[ones_bf])
        P.op("dve", lambda e: e.memset(epsc[:, 0:1], EPS), writes=[epsc])
        P.op("dve", lambda e: e.memset(epsc[:, 1:2], 1.0), writes=[epsc])
        P.op("pool", lambda e: e.memset(mtmp[:], 1.0), writes=[mtmp])
        P.op("pool", lambda e: e.affine_select(out=mtmp[:, 0, :], in_=mtmp[:, 0, :], pattern=[[-1, 128]], compare_op=ALU.is_equal,
                                               fill=0.0, base=0, channel_multiplier=1), reads=[mtmp], writes=[mtmp])
        P.op("act", lambda e: e.activation(out=ident_bf[:], in_=mtmp[:, 0, :], func=AF.Copy), reads=[mtmp], writes=[ident_bf])
        P.op("pool", lambda e: e.memset(mtmp[:], 1.0), reads=[mtmp], writes=[mtmp])
        P.op("pool", lambda e: e.affine_select(out=mtmp[:], in_=mtmp[:], pattern=[[0, 4], [1, 128]], compare_op=ALU.is_ge,
                                               fill=0.0, base=0, channel_multiplier=-1), reads=[mtmp], writes=[mtmp])
        P.op("act", lambda e: e.activation(out=mask2[:, 1, :, :], in_=mtmp[:], func=AF.Copy), reads=[mtmp], writes=[mask2])
        P.op("pool", lambda e: e.memset(mtmp[:], 1.0), reads=[mtmp], writes=[mtmp])
        P.op("pool", lambda e: e.affine_select(out=mtmp[:], in_=mtmp[:], pattern=[[0, 4], [-1, 128]], compare_op=ALU.is_ge,
                                               fill=0.0, base=-1, channel_multiplier=1), reads=[mtmp], writes=[mtmp])
        P.op("act", lambda e: e.activation(out=mask2[:, 0, :, :], in_=mtmp[:], func=AF.Copy), reads=[mtmp], writes=[mask2])
        P.op("pool", lambda e: e.memset(mtmp[:], 0.0), reads=[mtmp], writes=[mtmp])
        P.op("pool", lambda e: e.affine_select(out=mtmp[:], in_=mtmp[:], pattern=[[0, 4], [1, 128]], compare_op=ALU.is_ge,
                                               fill=-30000.0, base=0, channel_multiplier=-1), reads=[mtmp], writes=[mtmp])
        P.op("act", lambda e: e.activation(out=neg4[:], in_=mtmp[:], func=AF.Copy), reads=[mtmp], writes=[neg4])
        P.barrier()

        U_t = sb([128, 128], BF16, "U_t")
        arena0 = cur[0]
        P.op("act", lambda e: e.activation(out=U_t[:], in_=mask2[:, 1, 0, :], func=AF.Copy), reads=[mask2], writes=[U_t])
        P.barrier()
        Ucol = lambda: U_t[:]

        def load_w(dram_ap, n):
            wb = wbufs.next()
            P.op("pool", lambda e: e.dma_start(out=wb[:, 0:n], in_=dram_ap), writes=[wb], dma=True)
            return wb

        evac_i = [0]

        def evac_copy(out_ap, in_ap, reads, writes):
            evac_i[0] += 1
            if evac_i[0] % 2 == 0:
                P.op("act", lambda e: e.activation(out=out_ap, in_=in_ap, func=AF.Copy), reads=reads, writes=writes)
            else:
                P.op("dve", lambda e: e.tensor_copy(out=out_ap, in_=in_ap), reads=reads, writes=writes)

        def mm_acc(ps_ap, psbuf, pairs, extra_reads):
            n = len(pairs)
            for i, (l, r) in enumerate(pairs):
                P.op("pe", (lambda e, l=l, r=r, i=i: e.matmul(ps_ap, lhsT=l, rhs=r, start=(i == 0), stop=(i == n - 1))),
                     reads=extra_reads, writes=[psbuf])

        def rmsnorm_to_hT(lo, hi, wcol, nt):
            pieces = []
            a = lo
            while a < hi:
                b = min(hi, a + 512)
                pieces.append((a, b, psA.next()))
                a = b
            for c in range(16):
                sq = nt["sq"].next()
                P.op("act", (lambda e, c=c, sq=sq: e.activation(out=sq[:, 0:hi - lo], in_=xT[:, c, lo:hi], func=AF.Square)),
                     reads=[xT], writes=[sq])
                for (a, b, ps) in pieces:
                    P.op("pe", (lambda e, c=c, a=a, b=b, ps=ps, sq=sq: e.matmul(ps[:, 0:b - a], lhsT=ones_bf[:], rhs=sq[:, a - lo:b - lo],
                                                                              start=(c == 0), stop=(c == 15))),
                         reads=[sq, ones_bf], writes=[ps])
            rstd = nt["rstd"]
            for (a, b, ps) in pieces:
                P.op("act", (lambda e, a=a, b=b, ps=ps: e.activation(out=rstd[:, a:b], in_=ps[:, 0:b - a], func=AF.Sqrt,
                                                                    scale=1.0 / D, bias=epsc[:, 0:1])),
                     reads=[ps, epsc], writes=[rstd])
            P.op("dve", lambda e: e.reciprocal(out=rstd[:, lo:hi], in_=rstd[:, lo:hi]), reads=[rstd], writes=[rstd])
            return rstd

        def apply_norm(dst_fn, dstbuf, lo, hi, wcol, rstd):
            for c in range(16):
                P.op("dve", (lambda e, c=c: e.scalar_tensor_tensor(out=dst_fn(c), in0=xT[:, c, lo:hi], scalar=vec[:, wcol + c:wcol + c + 1],
                                                                  in1=rstd[:, lo:hi], op0=ALU.mult, op1=ALU.mult)),
                     reads=[xT, vec, rstd], writes=[dstbuf])

        MAIN = slice(HT, TT)

        try:
          for seg in range(NSEG):
            chk("const")
            P.barrier()
            cur[0] = arena0
            psA.i = 0
            psB.i = 0
            psT.i = 0
            wbufs.i = 0
            evac_i[0] = 0
            P.op("sp", (lambda e, seg=seg: e.dma_start(out=xT[:], in_=xT_d[seg].rearrange("(c p) t -> p c t", p=128))),
                 writes=[xT], dma=True)
            nt = dict(sq=Rot([sb([128, TT], BF16, "sq") for _ in range(2)]), rstd=sb([128, TT], F32, "rstd"))
            rstd = rmsnorm_to_hT(0, TT, LN1, nt)
            apply_norm(lambda c: hT[:, c, :], hT, 0, TT, LN1, rstd)
            P.barrier()
            cur[0] = arena0

            chk("norm1")
            Hl = sb([128, 2048], F32, "Hl")
            dtx = sb([128, 4, 32], F32, "dtx")
            dta = sb([128, 4, 32], F32, "dta")
            dtb = sb([128, 4, 32], F32, "dtb")
            dt_ = sb([128, 4, 32], F32, "dt")
            dtA = sb([128, 4, 32], F32, "dtA")
            dtA_bf = sb([128, 4, 32], BF16, "dtAbf")
            ndtA_bf = sb([128, 4, 32], BF16, "ndtAbf")
            acs = sb([128, 4, 32], F32, "acs")
            atot = sb([128, 4, 32], F32, "atot")
            wdec = sb([128, 4, 32], F32, "wdec")
            cdec = sb([128, 4, 32], F32, "cdec")
            dtw = sb([128, 4, 32], F32, "dtw")
            Aneg = sb([128, 32], F32, "Aneg")
            mark_ssd = cur[0]
            if p2:
                ssb = Rot([sb([128, 2048], F32, "ssb") for _ in range(2)])
                ddb = Rot([sb([128, 32], F32, "ddb") for _ in range(2)])
                P.op("dve", lambda e: e.memset(Hl[:], 0.0), writes=[Hl])
                for j in range(15):
                    s_ = ssb.next()
                    d_ = ddb.next()
                    P.op("sp", (lambda e, seg=seg, j=j, s_=s_: e.dma_start(out=s_[:], in_=SS_d[seg, j])), writes=[s_], dma=True)
                    P.op("sp", (lambda e, seg=seg, j=j, d_=d_: e.dma_start(out=d_[:], in_=DD_d[seg, j])), writes=[d_], dma=True)
                    P.op("dve", (lambda e, d_=d_: e.tensor_tensor(out=Hl[:].rearrange("p (h q) -> p h q", h=32),
                                                                 in0=Hl[:].rearrange("p (h q) -> p h q", h=32),
                                                                 in1=bc(d_[:], [128, 32, 64], 2), op=ALU.mult)),
                         reads=[Hl, d_], writes=[Hl])
                    P.op("dve", (lambda e, s_=s_: e.tensor_tensor(out=Hl[:], in0=Hl[:], in1=s_[:], op=ALU.add)),
                         reads=[Hl, s_], writes=[Hl])
                P.barrier()
                cur[0] = mark_ssd
            else:
                P.op("dve", lambda e: e.memset(Hl[:], 0.0), writes=[Hl])

            chk("hin")
            wdt = load_w(w_dt_d, 512)
            psd = psA.next()
            for c in range(4):
                for kc in range(16):
                    P.op("pe", (lambda e, c=c, kc=kc: e.matmul(psd[:, c * 32:(c + 1) * 32], lhsT=hT[:, kc, HT + c * 128:HT + (c + 1) * 128],
                                                             rhs=wdt[:, kc * 32:(kc + 1) * 32], start=(kc == 0), stop=(kc == 15))),
                         reads=[hT, wdt], writes=[psd])
            P.op("dve", lambda e: e.tensor_tensor(out=dtx[:], in0=psd[:, 0:128].rearrange("p (c h) -> p c h", c=4),
                                                  in1=bc(vec[:, DTB:DTB + 32], [128, 4, 32], 1), op=ALU.add),
                 reads=[psd, vec], writes=[dtx])
            chk("dt1")
            P.op("act", lambda e: e.activation(out=dta[:], in_=dtx[:], func=AF.Abs), reads=[dtx], writes=[dta])
            P.op("act", lambda e: e.activation(out=dta[:], in_=dta[:], func=AF.Exp, scale=-1.0), reads=[dta], writes=[dta])
            P.op("act", lambda e: e.activation(out=dta[:], in_=dta[:], func=AF.Ln, bias=epsc[:, 1:2], scale=1.0), reads=[dta, epsc], writes=[dta])
            P.op("dve", lambda e: e.tensor_scalar(out=dtb[:], in0=dtx[:], scalar1=0.0, scalar2=None, op0=ALU.max), reads=[dtx], writes=[dtb])
            P.op("dve", lambda e: e.tensor_tensor(out=dt_[:], in0=dtb[:], in1=dta[:], op=ALU.add), reads=[dtb, dta], writes=[dt_])
            chk("dt2")
            P.op("act", lambda e: e.activation(out=Aneg[:], in_=vec[:, ALOG:ALOG + 32], func=AF.Exp), reads=[vec], writes=[Aneg])
            P.op("dve", lambda e: e.tensor_scalar(out=Aneg[:], in0=Aneg[:], scalar1=-1.0, scalar2=None, op0=ALU.mult), reads=[Aneg], writes=[Aneg])
            P.op("dve", lambda e: e.tensor_tensor(out=dtA[:], in0=dt_[:], in1=bc(Aneg[:], [128, 4, 32], 1), op=ALU.mult),
                 reads=[dt_, Aneg], writes=[dtA])
            P.op("act", lambda e: e.activation(out=dtA_bf[:], in_=dtA[:], func=AF.Copy), reads=[dtA], writes=[dtA_bf])
            P.op("dve", lambda e: e.tensor_scalar(out=ndtA_bf[:], in0=dtA_bf[:], scalar1=-1.0, scalar2=None, op0=ALU.mult), reads=[dtA_bf], writes=[ndtA_bf])
            chk("dt3")
            import os as _os
            if _os.environ.get("SKIPPS"):
                psA.next()
            psd2 = psA.next()
            for c in range(4):
                P.op("pe", (lambda e, c=c: e.matmul(psd2[:, c * 32:(c + 1) * 32], lhsT=Ucol(), rhs=dtA_bf[:, c, :], start=True, stop=True)),
                     reads=[mask2, dtA_bf], writes=[psd2])
                P.op("pe", (lambda e, c=c: e.matmul(psd2[:, 128 + c * 32:128 + (c + 1) * 32], lhsT=ones_bf[:], rhs=dtA_bf[:, c, :], start=True, stop=True)),
                     reads=[ones_bf, dtA_bf], writes=[psd2])
            chk("dt3a")
            P.op("dve", lambda e: e.tensor_copy(out=acs[:], in_=psd2[:, 0:128].rearrange("p (c h) -> p c h", c=4)),
                 reads=[psd2], writes=[acs])
            P.op("dve", lambda e: e.tensor_copy(out=atot[:], in_=psd2[:, 128:256].rearrange("p (c h) -> p c h", c=4)),
                 reads=[psd2], writes=[atot])
            chk("dt3b")
            P.op("dve", lambda e: e.tensor_tensor(out=wdec[:], in0=atot[:], in1=acs[:], op=ALU.subtract), reads=[atot, acs], writes=[wdec])
            P.op("act", lambda e: e.activation(out=wdec[:], in_=wdec[:], func=AF.Exp), reads=[wdec], writes=[wdec])
            P.op("act", lambda e: e.activation(out=cdec[:], in_=atot[:], func=AF.Exp), reads=[atot], writes=[cdec])
            chk("dt3c")
            P.op("dve", lambda e: e.tensor_tensor(out=dtw[:], in0=dt_[:], in1=wdec[:], op=ALU.mult), reads=[dt_, wdec], writes=[dtw])
            chk("dt4")
            if not p2:
                dsum = sb([128, 32], F32, "dsum")
                P.op("dve", lambda e: e.tensor_tensor(out=dsum[:], in0=atot[:, 0, :], in1=atot[:, 1, :], op=ALU.add), reads=[atot], writes=[dsum])
                P.op("dve", lambda e: e.tensor_tensor(out=dsum[:], in0=dsum[:], in1=atot[:, 2, :], op=ALU.add), reads=[atot, dsum], writes=[dsum])
                P.op("dve", lambda e: e.tensor_tensor(out=dsum[:], in0=dsum[:], in1=atot[:, 3, :], op=ALU.add), reads=[atot, dsum], writes=[dsum])
                P.op("act", lambda e: e.activation(out=dsum[:], in_=dsum[:], func=AF.Exp), reads=[dsum], writes=[dsum])
                P.op("sp", (lambda e, seg=seg: e.dma_start(out=D_out_d[seg], in_=dsum[:])), reads=[dsum], writes=[Buf(None, "dout")], dma=True)

            chk("dtpre")
            xtmp = Rot([sb([128, TT], F32, "xtmp") for _ in range(2)])
            acc = Rot([sb([128, T], F32, "acc") for _ in range(2)])
            xg = sb([128, 6, T], BF16, "xg")
            xdt_g = sb([128, 4, T], BF16, "xdt")
            xdtw_g = sb([128, 4, T], BF16, "xdtw")
            Btok = sb([128, 4, 128], BF16, "Btok")
            if p2:
                prev_g = sb([128, 4, T], BF16, "prev")
                yT_g = sb([128, 4, T], F32, "yTg")
                CBT = Rot([sb([128, 128], F32, "CBT") for _ in range(2)])
                Rr = Rot([sb([128, 8, 128], BF16, "R") for _ in range(2)])
                expA = Rot([sb([128, 8, 128], BF16, "expA") for _ in range(2)])
                Ee = Rot([sb([128, 8, 128], BF16, "E") for _ in range(2)])
                MT = Rot([sb([128, 8, 128], BF16, "MT") for _ in range(2)])
                CsT = Rot([sb([128, 8, 128], BF16, "CsT") for _ in range(2)])
                szr = Rot([sb([128, T], F32, "sz") for _ in range(2)])
                sqr = Rot([sb([128, T], BF16, "sqz") for _ in range(2)])
                rstd_g = sb([128, T], F32, "rstdg")

            for g in range(4):
                for blk in range(3):
                    wb = load_w(w_ssd_d[g * 3 + blk], 4096)
                    for sub in range(2):
                        lf = 2 * blk + sub
                        gc = (4 * g + lf) if lf < 4 else (16 + g if lf == 4 else 20 + g)
                        xt = xtmp.next()
                        for (a, b) in [(0, HT), (HT, TT)]:
                            ps = psA.next()
                            mm_acc(ps[:, 0:b - a], ps, [(wb[:, kc * 256 + sub * 128: kc * 256 + (sub + 1) * 128], hT[:, kc, a:b]) for kc in range(16)],
                                   [wb, hT])
                            P.op("act", (lambda e, a=a, b=b, ps=ps, xt=xt: e.activation(out=xt[:, a:b], in_=ps[:, 0:b - a], func=AF.Copy)),
                                 reads=[ps], writes=[xt])
                        ac = acc.next()
                        P.op("dve", (lambda e, gc=gc, xt=xt, ac=ac: e.tensor_scalar(out=ac[:], in0=xt[:, HT:TT], scalar1=vec[:, CW + gc * 4 + 3:CW + gc * 4 + 4],
                                                                                  scalar2=vec[:, CB + gc:CB + gc + 1], op0=ALU.mult, op1=ALU.add)),
                             reads=[xt, vec], writes=[ac])
                        for k in range(3):
                            sh = 3 - k
                            P.op("dve", (lambda e, gc=gc, xt=xt, ac=ac, k=k, sh=sh: e.scalar_tensor_tensor(
                                out=ac[:], in0=xt[:, HT - sh:TT - sh], scalar=vec[:, CW + gc * 4 + k:CW + gc * 4 + k + 1], in1=ac[:],
                                op0=ALU.mult, op1=ALU.add)), reads=[xt, vec, ac], writes=[ac])
                        P.op("act", (lambda e, lf=lf, ac=ac: e.activation(out=xg[:, lf, :], in_=ac[:], func=AF.Silu)),
                             reads=[ac], writes=[(xg, lf)])
                chk("conv")
                for c in range(4):
                    pt = psT.next()
                    for lf in range(4):
                        P.op("pe", (lambda e, lf=lf, c=c, pt=pt: e.transpose(out=pt[:, lf * 128:(lf + 1) * 128], in_=xg[:, lf, c * 128:(c + 1) * 128],
                                                                            identity=ident_bf[:])), reads=[(xg, lf), ident_bf], writes=[pt])
                    P.op("dve", (lambda e, c=c, pt=pt, g=g: e.tensor_tensor(out=xdt_g[:, c, :].rearrange("p (h q) -> p h q", h=8),
                                                                           in0=pt[:].rearrange("p (h q) -> p h q", h=8),
                                                                           in1=bc(dt_[:, c, 8 * g:8 * g + 8], [128, 8, 64], 2), op=ALU.mult)),
                         reads=[pt, dt_], writes=[(xdt_g, c)])
                    P.op("dve", (lambda e, c=c, pt=pt, g=g: e.tensor_tensor(out=xdtw_g[:, c, :].rearrange("p (h q) -> p h q", h=8),
                                                                           in0=pt[:].rearrange("p (h q) -> p h q", h=8),
                                                                           in1=bc(dtw[:, c, 8 * g:8 * g + 8], [128, 8, 64], 2), op=ALU.mult)),
                         reads=[pt, dtw], writes=[(xdtw_g, c)])
                pt = psT.next()
                for c in range(4):
                    P.op("pe", (lambda e, c=c, pt=pt: e.transpose(out=pt[:, c * 128:(c + 1) * 128], in_=xg[:, 4, c * 128:(c + 1) * 128],
                                                                 identity=ident_bf[:])), reads=[(xg, 4), ident_bf], writes=[pt])
                P.op("act", (lambda e, pt=pt: e.activation(out=Btok[:].rearrange("p c n -> p (c n)"), in_=pt[:], func=AF.Copy)),
                     reads=[pt], writes=[Btok])
                chk("transp")
                Hg = lambda: Hl[:, g * 512:(g + 1) * 512]
                for c in range(4):
                    if p2:
                        P.op("act", (lambda e, c=c, g=g: e.activation(out=prev_g[:, c, :], in_=Hl[:, g * 512:(g + 1) * 512], func=AF.Copy)),
                             reads=[(Hl, g)], writes=[(prev_g, c)])
                    ps = psA.next()
                    P.op("pe", (lambda e, c=c, ps=ps: e.matmul(ps[:], lhsT=Btok[:, c, :], rhs=xdtw_g[:, c, :], start=True, stop=True)),
                         reads=[Btok, (xdtw_g, c)], writes=[ps])
                    P.op("dve", (lambda e, c=c, g=g: e.tensor_tensor(out=Hl[:, g * 512:(g + 1) * 512].rearrange("p (h q) -> p h q", h=8),
                                                                    in0=Hl[:, g * 512:(g + 1) * 512].rearrange("p (h q) -> p h q", h=8),
                                                                    in1=bc(cdec[:, c, 8 * g:8 * g + 8], [128, 8, 64], 2), op=ALU.mult)),
                         reads=[(Hl, g), cdec], writes=[(Hl, g)])
                    P.op("dve", (lambda e, ps=ps, g=g: e.tensor_tensor(out=Hl[:, g * 512:(g + 1) * 512], in0=Hl[:, g * 512:(g + 1) * 512],
                                                                      in1=ps[:], op=ALU.add)), reads=[(Hl, g), ps], writes=[(Hl, g)])
                chk("scan")
                if not p2:
                    continue
                for c in range(4):
                    cs = slice(c * 128, (c + 1) * 128)
                    ps = psA.next()
                    P.op("pe", (lambda e, ps=ps, cs=cs: e.matmul(ps[:, 0:128], lhsT=xg[:, 4, cs], rhs=xg[:, 5, cs], start=True, stop=True)),
                         reads=[(xg, 4), (xg, 5)], writes=[ps])
                    cb = CBT.next()
                    P.op("act", (lambda e, ps=ps, cb=cb: e.activation(out=cb[:], in_=ps[:, 0:128], func=AF.Copy)), reads=[ps], writes=[cb])
                    r_ = Rr.next()
                    P.op("dve", (lambda e, r_=r_, c=c, g=g: e.tensor_tensor(out=r_[:], in0=bc(Ucol(), [128, 8, 128], 1),
                                                                           in1=bc(dtA_bf[:, c, 8 * g:8 * g + 8], [128, 8, 128], 2), op=ALU.mult)),
                         reads=[mask2, dtA_bf], writes=[r_])
                    a1 = psB.next()
                    for hh in range(2):
                        P.op("pe", (lambda e, a1=a1, r_=r_, hh=hh: e.matmul(a1[:, hh * 512:(hh + 1) * 512], lhsT=ones_bf[:],
                                                                           rhs=r_[:, 4 * hh:4 * hh + 4, :], start=True, stop=True)),
                             reads=[ones_bf, r_], writes=[a1])
                    ea = expA.next()
                    P.op("act", (lambda e, a1=a1, ea=ea: e.activation(out=ea[:].rearrange("p h l -> p (h l)"), in_=a1[:], func=AF.Exp)),
                         reads=[a1], writes=[ea])
                    a2 = psB.next()
                    for hh in range(2):
                        P.op("pe", (lambda e, a2=a2, r_=r_, hh=hh: e.matmul(a2[:, hh * 512:(hh + 1) * 512], lhsT=ones_bf[:],
                                                                           rhs=r_[:, 4 * hh:4 * hh + 4, :], start=True, stop=False)),
                             reads=[ones_bf, r_], writes=[a2])
                        P.op("pe", (lambda e, a2=a2, hh=hh, c=c, g=g: e.matmul(a2[:, hh * 512:(hh + 1) * 512], lhsT=Ucol(),
                                                                              rhs=bc(ndtA_bf[:, c, 8 * g + 4 * hh:8 * g + 4 * hh + 4], [128, 4, 128], 2),
                                                                              start=False, stop=False)),
                             reads=[mask2, ndtA_bf], writes=[a2])
                        P.op("pe", (lambda e, a2=a2, hh=hh: e.matmul(a2[:, hh * 512:(hh + 1) * 512], lhsT=ident_bf[:], rhs=neg4[:],
                                                                    start=False, stop=True)), reads=[ident_bf, neg4], writes=[a2])
                    ee = Ee.next()
                    P.op("act", (lambda e, a2=a2, ee=ee: e.activation(out=ee[:].rearrange("p h l -> p (h l)"), in_=a2[:], func=AF.Exp)),
                         reads=[a2], writes=[ee])
                    mt = MT.next()
                    P.op("dve", (lambda e, mt=mt, ee=ee, cb=cb: e.tensor_tensor(out=mt[:], in0=ee[:], in1=bc(cb[:], [128, 8, 128], 1), op=ALU.mult)),
                         reads=[ee, cb], writes=[mt])
                    ct = CsT.next()
                    P.op("dve", (lambda e, ct=ct, ea=ea, cs=cs: e.tensor_tensor(out=ct[:], in0=ea[:], in1=bc(xg[:, 5, cs], [128, 8, 128], 1), op=ALU.mult)),
                         reads=[ea, (xg, 5)], writes=[ct])
                    yp = psB.next()
                    for hp in range(4):
                        P.op("pe", (lambda e, yp=yp, hp=hp, c=c, mt=mt: e.matmul(yp[:, hp * 256:(hp + 1) * 256], lhsT=xdt_g[:, c, hp * 128:(hp + 1) * 128],
                                                                                rhs=mt[:, 2 * hp:2 * hp + 2, :], start=True, stop=False)),
                             reads=[(xdt_g, c), mt], writes=[yp])
                        P.op("pe", (lambda e, yp=yp, hp=hp, c=c, ct=ct: e.matmul(yp[:, hp * 256:(hp + 1) * 256], lhsT=prev_g[:, c, hp * 128:(hp + 1) * 128],
                                                                                rhs=ct[:, 2 * hp:2 * hp + 2, :], start=False, stop=True)),
                             reads=[(prev_g, c), ct], writes=[yp])
                    for e_ in range(2):
                        rows = slice(e_ * 64, (e_ + 1) * 64)
                        P.op("dve", (lambda e, yp=yp, rows=rows, e_=e_, cs=cs: e.tensor_copy(
                            out=yT_g[rows, :, cs], in_=yp[rows, :].rearrange("p (a b) -> p a b", a=4)[:, :, e_ * 128:(e_ + 1) * 128])),
                             reads=[yp], writes=[(yT_g, c)])
                chk("ydiag")
                for hp in range(4):
                    fc = 4 * g + hp
                    P.op("dve", (lambda e, hp=hp, fc=fc: e.scalar_tensor_tensor(out=yT_g[:, hp, :], in0=xg[:, hp, :], scalar=vec[:, DSK + fc:DSK + fc + 1],
                                                                               in1=yT_g[:, hp, :], op0=ALU.mult, op1=ALU.add)),
                         reads=[(xg, hp), vec, yT_g], writes=[yT_g])
                nps = psB.next()
                for blk in range(2):
                    wb = load_w(w_z_d[g * 2 + blk], 4096)
                    for sub in range(2):
                        hp = 2 * blk + sub
                        ps = psA.next()
                        mm_acc(ps[:], ps, [(wb[:, kc * 256 + sub * 128:kc * 256 + (sub + 1) * 128], hT[:, kc, MAIN]) for kc in range(16)], [wb, hT])
                        sz = szr.next()
                        P.op("act", (lambda e, ps=ps, sz=sz: e.activation(out=sz[:], in_=ps[:], func=AF.Silu)), reads=[ps], writes=[sz])
                        P.op("dve", (lambda e, hp=hp, sz=sz: e.tensor_tensor(out=yT_g[:, hp, :], in0=yT_g[:, hp, :], in1=sz[:], op=ALU.mult)),
                             reads=[yT_g, sz], writes=[yT_g])
                        sq = sqr.next()
                        P.op("act", (lambda e, hp=hp, sq=sq: e.activation(out=sq[:], in_=yT_g[:, hp, :], func=AF.Square)), reads=[yT_g], writes=[sq])
                        P.op("pe", (lambda e, hp=hp, sq=sq, nps=nps: e.matmul(nps[:, 0:512], lhsT=ones_bf[:], rhs=sq[:], start=(hp == 0), stop=(hp == 3))),
                             reads=[ones_bf, sq], writes=[nps])
                P.op("act", (lambda e, nps=nps: e.activation(out=rstd_g[:], in_=nps[:, 0:512], func=AF.Sqrt, scale=1.0 / 512, bias=epsc[:, 0:1])),
                     reads=[nps, epsc], writes=[rstd_g])
                P.op("dve", lambda e: e.reciprocal(out=rstd_g[:], in_=rstd_g[:]), reads=[rstd_g], writes=[rstd_g])
                for hp in range(4):
                    fc = 4 * g + hp
                    P.op("dve", (lambda e, hp=hp, fc=fc: e.scalar_tensor_tensor(out=ssmT[:, fc, :], in0=yT_g[:, hp, :], scalar=vec[:, NRM + fc:NRM + fc + 1],
                                                                               in1=rstd_g[:], op0=ALU.mult, op1=ALU.mult)),
                         reads=[yT_g, vec, rstd_g], writes=[(ssmT, fc)])

            if not p2:
                P.op("sp", (lambda e, seg=seg: e.dma_start(out=S_out_d[seg], in_=Hl[:])), reads=[Hl], writes=[Buf(None, "sout")], dma=True)
                continue

            chk("ssd")
            P.barrier()
            cur[0] = arena0
            attT = sb([128, 8, T], BF16, "attT")
            poolT = sb([128, 8, T], BF16, "poolT")
            arena1 = cur[0]
            segv = sb([128, 65], F32, "segv")
            P.op("sp", (lambda e, seg=seg: e.dma_start(out=segv[:], in_=segv_d[seg])), writes=[segv], dma=True)
            arena1 = cur[0]

            qT = sb([128, 16, T], BF16, "qTz")
            P.op("dve", lambda e: e.memset(qT[:], 0.0), writes=[qT])
            kT = sb([128, 4, TT], BF16, "kT")
            v_sb = sb([128, 5, 512], BF16, "v")
            Pp = Rot([sb([128, 2, 512], BF16, "P") for _ in range(2)])
            den = Rot([sb([128, 4, 128], F32, "den") for _ in range(2)])
            sinkexp = sb([128, 16], F32, "sinkexp")
            P.op("act", lambda e: e.activation(out=sinkexp[:], in_=vec[:, SINK:SINK + 16], func=AF.Exp), reads=[vec], writes=[sinkexp])
            for blk in range(4):
                wb = load_w(w_q_d[blk], 4096)
                for sub in range(2):
                    qc = 2 * blk + sub
                    ps = psA.next()
                    mm_acc(ps[:], ps, [(wb[:, kc * 256 + sub * 128:kc * 256 + (sub + 1) * 128], hT[:, kc, MAIN]) for kc in range(16)], [wb, hT])
                    for e_ in range(2):
                        rws = slice(e_ * 64, e_ * 64 + 64)
                        evac_copy(qT[rws, 2 * qc + e_, :], ps[rws, :], [ps], [qT])
            for blk in range(2):
                wb = load_w(w_k_d[blk], 4096)
                for sub in range(2):
                    j = 2 * blk + sub
                    for (a, b) in [(0, HT), (HT, TT)]:
                        ps = psA.next()
                        mm_acc(ps[:, 0:b - a], ps, [(wb[:, kc * 256 + sub * 128:kc * 256 + (sub + 1) * 128], hT[:, kc, a:b]) for kc in range(16)], [wb, hT])
                        evac_copy(kT[:, j, a:b], ps[:, 0:b - a], [ps], [(kT, j)])
            for blk in range(2):
                wb = load_w(w_v_d[blk], 4096)
                for tch in range(5):
                    ps = psA.next()
                    mm_acc(ps[:, 0:256], ps, [(hT[:, kc, tch * 128:(tch + 1) * 128], wb[:, kc * 256:(kc + 1) * 256]) for kc in range(16)], [wb, hT])
                    evac_copy(v_sb[:, tch, blk * 256:(blk + 1) * 256], ps[:, 0:256], [ps], [v_sb])
            for b in range(1, 5):
                qs = slice((b - 1) * 128, b * 128)
                for j in range(4):
                    st_ = psB.next()
                    for g4 in range(4):
                        head = 4 * j + g4
                        qc = head
                        rows = slice(0, 128)
                        P.op("pe", (lambda e, st_=st_, g4=g4, rows=rows, j=j, qc=qc, b=b, qs=qs: e.matmul(
                            st_[:, g4 * 128:(g4 + 1) * 128], lhsT=kT[rows, j, (b - 1) * 128:b * 128], rhs=qT[rows, qc, qs], start=True, stop=True)),
                             reads=[kT, qT], writes=[st_])
                        P.op("pe", (lambda e, st_=st_, g4=g4, rows=rows, j=j, qc=qc, b=b, qs=qs: e.matmul(
                            st_[:, 512 + g4 * 128:512 + (g4 + 1) * 128], lhsT=kT[rows, j, b * 128:(b + 1) * 128], rhs=qT[rows, qc, qs], start=True, stop=True)),
                             reads=[kT, qT], writes=[st_])
                    pp = Pp.next()
                    P.op("act", (lambda e, st_=st_, pp=pp: e.activation(out=pp[:].rearrange("p a b -> p (a b)"), in_=st_[:], func=AF.Exp, scale=0.125)),
                         reads=[st_], writes=[pp])
                    P.op("dve", (lambda e, pp=pp: e.tensor_tensor(out=pp[:].rearrange("p a b -> p (a b)"), in0=pp[:].rearrange("p a b -> p (a b)"),
                                                                 in1=mask2[:].rearrange("p a g q -> p (a g q)"), op=ALU.mult)),
                         reads=[pp, mask2], writes=[pp])
                    if b == 1:
                        P.op("dve", (lambda e, pp=pp: e.tensor_scalar(out=pp[:, 0, :], in0=pp[:, 0, :], scalar1=segv[:, 0:1], scalar2=None, op0=ALU.mult)),
                             reads=[pp, segv], writes=[pp])
                    ad = psB.next()
                    P.op("pe", (lambda e, ad=ad, pp=pp, b=b, j=j: e.matmul(ad[:, 0:512], lhsT=v_sb[:, b - 1, j * 128:(j + 1) * 128], rhs=pp[:, 0, :],
                                                                          start=True, stop=False)), reads=[v_sb, pp], writes=[ad])
                    P.op("pe", (lambda e, ad=ad, pp=pp, b=b, j=j: e.matmul(ad[:, 0:512], lhsT=v_sb[:, b, j * 128:(j + 1) * 128], rhs=pp[:, 1, :],
                                                                          start=False, stop=True)), reads=[v_sb, pp], writes=[ad])
                    P.op("pe", (lambda e, ad=ad, pp=pp: e.matmul(ad[:, 512:1024], lhsT=ones_bf[:], rhs=pp[:, 0, :], start=True, stop=False)),
                         reads=[ones_bf, pp], writes=[ad])
                    P.op("pe", (lambda e, ad=ad, pp=pp: e.matmul(ad[:, 512:1024], lhsT=ones_bf[:], rhs=pp[:, 1, :], start=False, stop=True)),
                         reads=[ones_bf, pp], writes=[ad])
                    dn = den.next()
                    for g4 in range(4):
                        P.op("dve", (lambda e, ad=ad, dn=dn, g4=g4, j=j: e.tensor_scalar(out=dn[:, g4, :], in0=ad[:, 512 + g4 * 128:512 + (g4 + 1) * 128],
                                                                                        scalar1=sinkexp[:, 4 * j + g4:4 * j + g4 + 1], scalar2=None, op0=ALU.add)),
                             reads=[ad, sinkexp], writes=[dn])
                    P.op("dve", (lambda e, dn=dn: e.reciprocal(out=dn[:], in_=dn[:])), reads=[dn], writes=[dn])
                    for e_ in range(2):
                        rows = slice(e_ * 64, e_ * 64 + 64)
                        P.op("dve", (lambda e, ad=ad, dn=dn, rows=rows, e_=e_, j=j, qs=qs: e.tensor_tensor(
                            out=attT[rows, 2 * j:2 * j + 2, qs],
                            in0=ad[rows, 0:512].rearrange("p (g q) -> p g q", g=4)[:, 2 * e_:2 * e_ + 2, :],
                            in1=dn[rows, 2 * e_:2 * e_ + 2, :], op=ALU.mult)), reads=[ad, dn], writes=[attT])

            chk("attn")
            P.barrier()
            cur[0] = arena1
            uT = sb([128, 8, TT], F32, "uT")
            tA = sb([128, TT], F32, "tA")
            tB = sb([128, TT], F32, "tB")
            mixedT = sb([128, 8, T], BF16, "mixedT")
            t16 = sb([128, 16], F32, "t16")
            for blk in range(4):
                wb = load_w(w_u_d[blk], 4096)
                for sub in range(2):
                    uc = 2 * blk + sub
                    for (a, b) in [(0, HT), (HT, TT)]:
                        ps = psA.next()
                        mm_acc(ps[:, 0:b - a], ps, [(wb[:, kc * 256 + sub * 128:kc * 256 + (sub + 1) * 128], hT[:, kc, a:b]) for kc in range(16)], [wb, hT])
                        evac_copy(uT[:, uc, a:b], ps[:, 0:b - a], [ps], [(uT, uc)])
            for uc in range(8):
                gi = uc // 2
                w_ = 2 ** (gi + 1)
                tabs = [tA, tB]
                src = None
                lo = 64
                for st in range(1, gi + 2):
                    dst = tabs[(st - 1) % 2]
                    if st == 1:
                        P.op("dve", (lambda e, uc=uc, dst=dst: e.tensor_tensor(out=dst[:, 64:TT], in0=uT[:, uc, 64:TT], in1=uT[:, uc, 63:TT - 1], op=ALU.add)),
                             reads=[(uT, uc)], writes=[dst])
                    else:
                        d_ = 2 ** (st - 1)
                        lo2 = lo + d_
                        P.op("dve", (lambda e, dst=dst, s_=src, lo2=lo2, d_=d_: e.tensor_tensor(out=dst[:, lo2:TT], in0=s_[:, lo2:TT], in1=s_[:, lo2 - d_:TT - d_],
                                                                                             op=ALU.add)), reads=[src], writes=[dst])
                        lo = lo2
                    src = dst
                P.op("dve", (lambda e, uc=uc, src=src, w_=w_: e.scalar_tensor_tensor(out=mixedT[:, uc, :], in0=src[:, HT:TT], scalar=1.0 / w_,
                                                                                   in1=uT[:, uc, HT:TT], op0=ALU.mult, op1=ALU.subtract)),
                     reads=[src, (uT, uc)], writes=[(mixedT, uc)])
                P.op("dve", (lambda e, src=src, gi=gi: e.tensor_tensor(out=t16[:], in0=src[:, HT:HT + 16], in1=segv[:, 1 + gi * 16:1 + (gi + 1) * 16], op=ALU.mult)),
                     reads=[src, segv], writes=[t16])
                P.op("dve", (lambda e, uc=uc: e.tensor_tensor(out=mixedT[:, uc, 0:16], in0=t16[:], in1=uT[:, uc, HT:HT + 16], op=ALU.subtract)),
                     reads=[t16, (uT, uc)], writes=[(mixedT, uc)])
            wb = load_w(w_pool_d, 2048)
            for gi in range(4):
                for sub in range(2):
                    oc = 2 * gi + sub
                    ps = psA.next()
                    mm_acc(ps[:], ps, [(wb[:, gi * 512 + kc * 256 + sub * 128:gi * 512 + kc * 256 + (sub + 1) * 128], mixedT[:, 2 * gi + kc, :]) for kc in range(2)],
                           [wb, mixedT])
                    P.op("dve", (lambda e, ps=ps, oc=oc: e.tensor_scalar(out=poolT[:, oc, :], in0=ps[:], scalar1=vec[:, PSC + oc:PSC + oc + 1], scalar2=None,
                                                                        op0=ALU.mult)), reads=[ps, vec], writes=[poolT])

            chk("pool")
            P.barrier()
            cur[0] = arena1
            mergedT = sb([128, 16, T], BF16, "mergedT")
            gsr = Rot([sb([128, T], F32, "gs") for _ in range(3)])
            maccr = Rot([sb([128, T], F32, "macc") for _ in range(2)])
            tmr = Rot([sb([128, T], F32, "tm") for _ in range(2)])
            brsrc = [(attT, 8, 0), (poolT, 8, 8), (ssmT, 16, 16)]
            for fo in range(16):
                wg = load_w(w_gate_d[fo], 6144)
                wbr = load_w(w_br_d[fo], 4096)
                macc = maccr.next()
                for br in range(3):
                    gps = psA.next()
                    mm_acc(gps[:], gps, [(wg[:, kc * 384 + br * 128:kc * 384 + (br + 1) * 128], hT[:, kc, MAIN]) for kc in range(16)], [wg, hT])
                    gs = gsr.next()
                    P.op("act", (lambda e, gps=gps, gs=gs: e.activation(out=gs[:], in_=gps[:], func=AF.Sigmoid)), reads=[gps], writes=[gs])
                    src, nk, k0 = brsrc[br]
                    pps = psA.next()
                    mm_acc(pps[:], pps, [(wbr[:, (k0 + kc) * 128:(k0 + kc + 1) * 128], src[:, kc, :]) for kc in range(nk)], [wbr, src])
                    if br == 0:
                        P.op("dve", (lambda e, pps=pps, gs=gs, macc=macc: e.tensor_tensor(out=macc[:], in0=pps[:], in1=gs[:], op=ALU.mult)),
                             reads=[pps, gs], writes=[macc])
                    else:
                        tm = tmr.next()
                        P.op("dve", (lambda e, pps=pps, gs=gs, tm=tm: e.tensor_tensor(out=tm[:], in0=pps[:], in1=gs[:], op=ALU.mult)),
                             reads=[pps, gs], writes=[tm])
                        if br == 1:
                            P.op("dve", (lambda e, tm=tm, macc=macc: e.tensor_tensor(out=macc[:], in0=macc[:], in1=tm[:], op=ALU.add)),
                                 reads=[macc, tm], writes=[macc])
                        else:
                            P.op("dve", (lambda e, tm=tm, macc=macc, fo=fo: e.tensor_tensor(out=mergedT[:, fo, :], in0=macc[:], in1=tm[:], op=ALU.add)),
                                 reads=[macc, tm], writes=[(mergedT, fo)])
            chk("merge")
            for blk in range(8):
                wb = load_w(w_o_d[blk], 4096)
                for sub in range(2):
                    fo = 2 * blk + sub
                    ps = psA.next()
                    mm_acc(ps[:], ps, [(wb[:, kc * 256 + sub * 128:kc * 256 + (sub + 1) * 128], mergedT[:, kc, :]) for kc in range(16)], [wb, mergedT])
                    P.op("dve", (lambda e, ps=ps, fo=fo: e.tensor_tensor(out=xT[:, fo, MAIN], in0=xT[:, fo, MAIN], in1=ps[:], op=ALU.add)),
                         reads=[ps, (xT, fo)], writes=[(xT, fo)])
            chk("wout")
            P.barrier()
            cur[0] = arena0
            actT = sb([128, 44, T], BF16, "actT")
            sgr = Rot([sb([128, T], F32, "sg") for _ in range(2)])
            nt = dict(sq=Rot([sb([128, TT], BF16, "sq2") for _ in range(2)]), rstd=sb([128, TT], F32, "rstd2"))
            rstd = rmsnorm_to_hT(HT, TT, LN2, nt)
            apply_norm(lambda c: hT[:, c, MAIN], hT, HT, TT, LN2, rstd)
            chk("ffn_norm")
            for ffc in range(44):
                wb = load_w(w_gu_d[ffc], 4096)
                gps = psA.next()
                mm_acc(gps[:], gps, [(wb[:, kc * 256:kc * 256 + 128], hT[:, kc, MAIN]) for kc in range(16)], [wb, hT])
                ups = psA.next()
                mm_acc(ups[:], ups, [(wb[:, kc * 256 + 128:kc * 256 + 256], hT[:, kc, MAIN]) for kc in range(16)], [wb, hT])
                sg = sgr.next()
                P.op("act", (lambda e, gps=gps, sg=sg: e.activation(out=sg[:], in_=gps[:], func=AF.Silu)), reads=[gps], writes=[sg])
                P.op("dve", (lambda e, ups=ups, sg=sg, ffc=ffc: e.tensor_tensor(out=actT[:, ffc, :], in0=sg[:], in1=ups[:], op=ALU.mult)),
                     reads=[sg, ups], writes=[(actT, ffc)])
            chk("ffn_gu")
            for fo in range(16):
                wb = load_w(w_dn_d[fo], 5632)
                ps = psA.next()
                mm_acc(ps[:], ps, [(wb[:, kc * 128:(kc + 1) * 128], actT[:, kc, :]) for kc in range(44)], [wb, actT])
                P.op("dve", (lambda e, ps=ps, fo=fo: e.tensor_tensor(out=xT[:, fo, MAIN], in0=xT[:, fo, MAIN], in1=ps[:], op=ALU.add)),
                     reads=[ps, (xT, fo)], writes=[(xT, fo)])
            chk("ffn_dn")
            if last:
                rstd = rmsnorm_to_hT(HT, TT, FIN, nt)
                apply_norm(lambda c: xT[:, c, MAIN], xT, HT, TT, FIN, rstd)
            P.op("sp", (lambda e, seg=seg: e.dma_start(out=y_d[seg].rearrange("(c p) t -> p c t", p=128), in_=xT[:, :, MAIN])),
                 reads=[xT], writes=[Buf(None, "yout")], dma=True)

            chk("seg_end")
        except StopBuild:
            pass
        P.emit()
    return nc


def tile_w(W, FB):
    K, F_ = W.shape
    KC = K // 128
    nb = F_ // FB
    return np.ascontiguousarray(W.reshape(KC, 128, nb, FB).transpose(2, 1, 0, 3).reshape(nb, 128, KC * FB))


def pm(v):
    return np.ascontiguousarray(v.reshape(-1, 128).T)


def prep_layer(inp, i):
    w_in = inp["w_in"][i]
    q, k, v, u = w_in[:, 0:1024], w_in[:, 1024:1280], w_in[:, 1280:1536], w_in[:, 1536:2560]
    z = w_in[:, 2560:4608]
    xs, Bm, Cm = w_in[:, 4608:6656], w_in[:, 6656:7168], w_in[:, 7168:7680]
    dtw = w_in[:, 7680:7712]
    gates = w_in[:, 7712:13856]
    d = {}
    ssd_cols = []
    for g in range(4):
        ssd_cols += [xs[:, g * 512:(g + 1) * 512], Bm[:, g * 128:(g + 1) * 128], Cm[:, g * 128:(g + 1) * 128]]
    d["w_ssd"] = tile_w(np.concatenate(ssd_cols, axis=1), 256)
    d["w_dt"] = tile_w(dtw, 32)[0]
    d["w_z"] = tile_w(z, 256)
    d["w_q"] = tile_w(q, 256)
    kd = np.concatenate([np.concatenate([k[:, j * 64:(j + 1) * 64]] * 2, axis=1) for j in range(4)], axis=1)
    vd = np.concatenate([np.concatenate([v[:, j * 64:(j + 1) * 64]] * 2, axis=1) for j in range(4)], axis=1)
    d["w_k"] = tile_w(kd, 256)
    d["w_v"] = tile_w(vd, 256)
    d["w_u"] = tile_w(u, 256)
    gcols = gates.reshape(2048, 3, 16, 128).transpose(0, 2, 1, 3).reshape(2048, 16 * 384)
    d["w_gate"] = tile_w(gcols, 384)
    perm = np.zeros(1024, dtype=np.int64)
    for t in range(8):
        j, r = t // 2, t % 2
        for p in range(128):
            perm[t * 128 + p] = (4 * j + r + 2 * (p // 64)) * 64 + p % 64
    wbr = np.concatenate([inp["w_attn_br"][i][perm], inp["w_pool_br"][i], inp["w_ssm_br"][i]], axis=0)
    d["w_br"] = tile_w(wbr, 128)
    d["w_o"] = tile_w(inp["w_out"][i], 256)
    gu = inp["w_gate_up"][i]
    gucols = gu.reshape(2048, 2, 44, 128).transpose(0, 2, 1, 3).reshape(2048, 44 * 256)
    d["w_gu"] = tile_w(gucols, 256)
    d["w_dn"] = tile_w(inp["w_down"][i], 128)
    pw = inp["pool_w"][i]
    d["w_pool"] = np.ascontiguousarray(pw.reshape(4, 2, 128, 256).transpose(2, 0, 1, 3).reshape(128, 2048))
    vec = np.zeros((128, NV), np.float32)
    vec[:, LN1:LN1 + 16] = pm(inp["ln1_w"][i])
    vec[:, LN2:LN2 + 16] = pm(inp["ln2_w"][i])
    vec[:, FIN:FIN + 16] = pm(inp["final_w"])
    cw = inp["conv_w"][i]
    vec[:, CW:CW + 96] = cw.reshape(4, 24, 128).transpose(2, 1, 0).reshape(128, 96)
    vec[:, CB:CB + 24] = pm(inp["conv_b"][i])
    vec[:, DSK:DSK + 16] = pm(np.repeat(inp["d_skip"][i], 64))
    vec[:, NRM:NRM + 16] = pm(inp["ssm_norm_w"][i])
    vec[:, PSC:PSC + 8] = pm(inp["pool_scale"][i])
    vec[:, DTB:DTB + 32] = inp["dt_bias"][i][None, :]
    vec[:, ALOG:ALOG + 32] = inp["a_log"][i][None, :]
    vec[:, SINK:SINK + 16] = inp["attn_sink"][i][None, :]
    d["vec"] = vec
    return d


_prog_cache = {}


def get_prog(mode, last):
    key = (mode, last)
    if key not in _prog_cache:
        _prog_cache[key] = build_program(mode, last)
    return _prog_cache[key]


def seg_inputs(xfull):
    out = []
    for s in range(16):
        lo = s * T
        if s == 0:
            halo = np.zeros((HT, D), np.float32)
        else:
            halo = xfull[lo - HT:lo]
        out.append(np.ascontiguousarray(np.concatenate([halo, xfull[lo:lo + T]], axis=0).T))
    return out


def p2_inmaps(segs, S, Dd, wl):
    icnt = np.zeros((16, 4, 16), np.float32)
    for s in range(16):
        for gi in range(4):
            w = 2 ** (gi + 1)
            for t in range(16):
                icnt[s, gi, t] = 1.0 / min(t + 1, w) if s == 0 else 1.0 / w
    in_maps = []
    wkeys = ["vec", "w_ssd", "w_dt", "w_z", "w_q", "w_k", "w_v", "w_u", "w_gate", "w_br", "w_o", "w_gu", "w_dn", "w_pool"]
    for c in range(NCORE):
        SS = np.zeros((NSEG, 15, 128, 2048), np.float32)
        DD = np.ones((NSEG, 15, 128, 32), np.float32)
        segv = np.zeros((NSEG, 128, 65), np.float32)
        for i in range(NSEG):
            s = 2 * c + i
            for j in range(min(s, 15)):
                SS[i, j] = S[j]
                DD[i, j] = Dd[j]
            segv[i, :, 0] = 0.0 if s == 0 else 1.0
            segv[i, :, 1:65] = icnt[s].reshape(1, 64)
        m = dict(xT=np.stack([segs[2 * c], segs[2 * c + 1]]), SS=SS, DD=DD, segv=segv)
        for k in wkeys:
            m[k] = wl[k]
        in_maps.append(m)
    return in_maps


def run_layer(xfull, wl, last):
    segs = seg_inputs(xfull)
    nc1 = get_prog("p1", False)
    in_maps = []
    for c in range(NCORE):
        in_maps.append(dict(xT=np.stack([segs[2 * c], segs[2 * c + 1]]), vec=wl["vec"], w_ssd=wl["w_ssd"], w_dt=wl["w_dt"]))
    res = run_bass_kernel_spmd(nc1, in_maps, core_ids=list(range(NCORE)))
    S = [res.results[s // 2]["Sout"][s % 2] for s in range(16)]
    Dd = [res.results[s // 2]["Dout"][s % 2] for s in range(16)]
    nc2 = get_prog("p2", last)
    in_maps = p2_inmaps(segs, S, Dd, wl)
    res = run_bass_kernel_spmd(nc2, in_maps, core_ids=list(range(NCORE)))
    out = np.zeros((8192, D), np.float32)
    for s in range(16):
        out[s * T:(s + 1) * T] = res.results[s // 2]["yT"][s % 2].T
    return out


def kernel(**inputs):
    inp = {k: np.asarray(v, dtype=np.float32) for k, v in inputs.items()}
    x = inp["x"][0]
    for i in range(2):
        wl = prep_layer(inp, i)
        x = run_layer(x, wl, last=(i == 1))
    return x[None].astype(np.float32)
```
